# Optimizing a Trainium2 kernel written in Bass

```python
import math
import jax
import jax.numpy as jnp
from jax import lax
import numpy as np

D_MODEL = 2048
BATCH = 4
SEQ = 2048
DEPTH = 4
DEC_BATCH = 128
DEC_SEQ = 4
PAST_LEN = 16384
PAGE_SIZE = 128

N_MIXERS = 2
N_A = (DEPTH + 1) // 2
N_B = DEPTH // 2
M_HEADS = 8
M_DV = D_MODEL // M_HEADS
M_DK = M_DV // 2
M_CHUNK = 64
HK = M_HEADS * M_DK
HV = M_HEADS * M_DV
A_IN = 2 * HK + 2 * HV + 2 * M_HEADS
S_GROUPS = 8
S_CHUNK = 128
S_WIDTH = D_MODEL
S_DG = S_WIDTH // S_GROUPS
P_HEADS = 8
P_NKEYS = 128
P_NEXP = P_NKEYS * P_NKEYS
P_DQ = 256
P_DH = P_DQ // 2
P_TOPK = 16
P_BLOCK = 128
PLE_DIM = 256
ALPHA = (2 * DEPTH) ** 0.25
BETA = (8 * DEPTH) ** -0.25
LN_EPS = 1e-5

kernel_name = 'mlstm_chunkmlp_peer_hybrid_step'


def layer_norm(x, g, b):
    xf = x.astype(jnp.float32)
    mu = jnp.mean(xf, -1, keepdims=True)
    var = jnp.mean(jnp.square(xf - mu), -1, keepdims=True)
    y = (xf - mu) * lax.rsqrt(var + LN_EPS)
    return (y * g.astype(jnp.float32) + b.astype(jnp.float32)).astype(x.dtype)


def gain_norm(x, g):
    xf = x.astype(jnp.float32)
    mu = jnp.mean(xf, -1, keepdims=True)
    var = jnp.mean(jnp.square(xf - mu), -1, keepdims=True)
    return (xf - mu) * lax.rsqrt(var + LN_EPS) * g.astype(jnp.float32)


def _to_chunks(a, nc, L):
    B = a.shape[0]
    a = a.reshape((B, nc, L) + a.shape[2:])
    return jnp.swapaxes(jnp.moveaxis(a, 1, 0), 2, 3)


def mlstm_scan(q, k, v, ig, lf, C0, n0, m0):
    B, S = q.shape[:2]
    L = math.gcd(S, M_CHUNK)
    nc = S // L
    causal = jnp.tril(jnp.ones((L, L), dtype=bool))

    def step(carry, xs):
        C, n, m = carry
        qc, kc, vc, ic, fc = xs
        b = jnp.cumsum(fc, axis=-1)
        logd = jnp.where(causal, b[..., :, None] - b[..., None, :] + ic[..., None, :], -jnp.inf)
        inter = b + m[..., None]
        m_row = jnp.maximum(inter, jnp.max(logd, axis=-1))
        s = jnp.einsum('bhld,bhsd->bhls', qc, kc) * jnp.exp(logd - m_row[..., None])
        w_inter = jnp.exp(inter - m_row)
        num = w_inter[..., None] * jnp.einsum('bhld,bhde->bhle', qc, C) + jnp.einsum('bhls,bhse->bhle', s, vc)
        den = w_inter * jnp.einsum('bhld,bhd->bhl', qc, n) + jnp.sum(s, axis=-1)
        h = num / jnp.maximum(jnp.abs(den), jnp.exp(-m_row))[..., None]
        m_new = m_row[..., -1]
        g_state = jnp.exp(b[..., -1] + m - m_new)
        g_tok = jnp.exp(b[..., -1:] - b + ic - m_new[..., None])
        C_new = g_state[..., None, None] * C + jnp.einsum('bhl,bhld,bhle->bhde', g_tok, kc, vc)
        n_new = g_state[..., None] * n + jnp.einsum('bhl,bhld->bhd', g_tok, kc)
        return (C_new, n_new, m_new), h

    xs = tuple(_to_chunks(a, nc, L) for a in (q, k, v, ig, lf))
    (C, n, m), h = lax.scan(step, (C0, n0, m0), xs)
    h = jnp.transpose(h, (1, 0, 3, 2, 4)).reshape(B, S, M_HEADS, M_DV)
    return h, C, n, m


def mlstm_mixer(x, C0, n0, m0, w_in, b_gate, norm_w, w_out):
    B, S, _ = x.shape
    proj = x @ w_in
    cuts = [HK, 2 * HK, 2 * HK + HV, 2 * HK + 2 * HV, 2 * HK + 2 * HV + M_HEADS]
    q, k, v, o, gi, gf = jnp.split(proj, cuts, axis=-1)
    q = q.reshape(B, S, M_HEADS, M_DK).astype(jnp.float32) * (M_DK ** -0.5)
    k = k.reshape(B, S, M_HEADS, M_DK).astype(jnp.float32)
    v = v.reshape(B, S, M_HEADS, M_DV).astype(jnp.float32)
    ig = gi.astype(jnp.float32) + b_gate[:M_HEADS].astype(jnp.float32)
    lf = jax.nn.log_sigmoid(gf.astype(jnp.float32) + b_gate[M_HEADS:].astype(jnp.float32))
    h, C, n, m = mlstm_scan(q, k, v, ig, lf, C0.astype(jnp.float32), n0.astype(jnp.float32), m0.astype(jnp.float32))
    h = gain_norm(h, norm_w.reshape(M_HEADS, M_DV))
    h = h.reshape(B, S, HV).astype(x.dtype) * jax.nn.sigmoid(o)
    return h @ w_out, C, n, m


def chunk_mlp_mixer(x, w_in, b_in, vnorm_w, w_s, b_s, w_out):
    B, S, _ = x.shape
    L = min(S, S_CHUNK)
    nc = -(-S // L)
    hdn = jax.nn.gelu(x @ w_in + b_in)
    u, v = jnp.split(hdn, 2, axis=-1)
    v = gain_norm(v, vnorm_w).astype(x.dtype)
    vg = jnp.pad(v, ((0, 0), (0, nc * L - S), (0, 0))).reshape(B, nc, L, S_GROUPS, S_DG)
    ws = jnp.tril(w_s[:, :L, :L])
    mixed = jnp.einsum('gts,bcsgd->bctgd', ws, vg) + b_s[:, :L].T[None, None, :, :, None]
    mixed = mixed.reshape(B, nc * L, S_WIDTH)[:, :S]
    return (u * mixed) @ w_out, v


def peer_ffn(x, w_q, sub_keys, u_tab, v_tab):
    B, S, D = x.shape
    T = B * S
    xt = x.reshape(T, D)
    q = (xt @ w_q).reshape(T, P_HEADS, 2, P_DH).astype(jnp.float32)
    sc = jnp.einsum('thcd,ckd->thck', q, sub_keys.astype(jnp.float32))
    sv, si = lax.top_k(sc, P_TOPK)
    cand = (sv[:, :, 0, :, None] + sv[:, :, 1, None, :]).reshape(T, P_HEADS, P_TOPK * P_TOPK)
    cidx = (si[:, :, 0, :, None] * P_NKEYS + si[:, :, 1, None, :]).reshape(T, P_HEADS, P_TOPK * P_TOPK)
    top_s, pos = lax.top_k(cand, P_TOPK)
    eidx = jnp.take_along_axis(cidx, pos, axis=-1)
    gate = jax.nn.softmax(top_s, axis=-1)
    blk = min(P_BLOCK, T)
    nb = -(-T // blk)
    pad = nb * blk - T
    xb = jnp.pad(xt, ((0, pad), (0, 0))).reshape(nb, blk, D)
    eb = jnp.pad(eidx, ((0, pad), (0, 0), (0, 0))).reshape(nb, blk, P_HEADS, P_TOPK)
    gb = jnp.pad(gate, ((0, pad), (0, 0), (0, 0))).reshape(nb, blk, P_HEADS, P_TOPK)

    def block(args):
        xs, es, gs = args
        act = jax.nn.gelu(jnp.einsum('td,thkd->thk', xs, u_tab[es]).astype(jnp.float32)) * gs
        return jnp.einsum('thk,thkd->td', act.astype(xs.dtype), v_tab[es])

    out = lax.map(block, (xb, eb, gb)).reshape(nb * blk, D)[:T]
    return out.reshape(B, S, D)


def trunk(x, p, C_in, n_in, m_in, w_a_in, b_a_gate, a_norm_w, w_a_out, w_b_in, b_b_in, b_norm_w, w_b_s, b_b_s, w_b_out, ln_g, ln_b, peer_wq, peer_keys, peer_u, peer_v, ple_w, ple_gate_w):
    Cs, ns, ms, vs = [], [], [], []
    for i in range(DEPTH):
        j = i // N_MIXERS
        if i % N_MIXERS == 0:
            y, C, n, m = mlstm_mixer(x, C_in[j], n_in[j], m_in[j], w_a_in[j], b_a_gate[j], a_norm_w[j], w_a_out[j])
            Cs.append(C)
            ns.append(n)
            ms.append(m)
        else:
            y, v = chunk_mlp_mixer(x, w_b_in[j], b_b_in[j], b_norm_w[j], w_b_s[j], b_b_s[j], w_b_out[j])
            vs.append(v)
        x = layer_norm(ALPHA * x + y, ln_g[i, 0], ln_b[i, 0])
        x = layer_norm(ALPHA * x + peer_ffn(x, peer_wq[i], peer_keys[i], peer_u[i], peer_v[i]), ln_g[i, 1], ln_b[i, 1])
        x = x + jax.nn.sigmoid(x @ ple_gate_w[i]) * (p[i].astype(x.dtype) @ ple_w[i])
    return x, jnp.stack(Cs), jnp.stack(ns), jnp.stack(ms), jnp.stack(vs)


def setup_inputs(seed: int = 0) -> dict:
    key = jax.random.key(seed)
    ks = iter(jax.random.split(key, 40))

    def nrm(shape, scale):
        return jax.random.normal(next(ks), shape, jnp.float32) * scale

    f_bias = jax.random.uniform(next(ks), (N_A, M_HEADS), jnp.float32, 3.0, 6.0)
    i_bias = nrm((N_A, M_HEADS), 0.1)
    return {
        'x_prompt': nrm((BATCH, SEQ, D_MODEL), 1.0),
        'x_sample': nrm((DEC_BATCH, DEC_SEQ, D_MODEL), 1.0),
        'state_C': nrm((N_A, DEC_BATCH, M_HEADS, M_DK, M_DV), 0.1),
        'state_n': nrm((N_A, DEC_BATCH, M_HEADS, M_DK), 0.5),
        'state_m': nrm((N_A, DEC_BATCH, M_HEADS), 1.0),
        'p_prompt': nrm((DEPTH, BATCH, SEQ, PLE_DIM), 1.0),
        'p_sample': nrm((DEPTH, DEC_BATCH, DEC_SEQ, PLE_DIM), 1.0),
        'w_a_in': nrm((N_A, D_MODEL, A_IN), D_MODEL ** -0.5),
        'b_a_gate': jnp.concatenate([i_bias, f_bias], axis=-1),
        'a_norm_w': 1.0 + nrm((N_A, HV), 0.02),
        'w_a_out': nrm((N_A, HV, D_MODEL), BETA * HV ** -0.5),
        'w_b_in': nrm((N_B, D_MODEL, 2 * S_WIDTH), D_MODEL ** -0.5),
        'b_b_in': nrm((N_B, 2 * S_WIDTH), 0.02),
        'b_norm_w': 1.0 + nrm((N_B, S_WIDTH), 0.02),
        'w_b_s': nrm((N_B, S_GROUPS, S_CHUNK, S_CHUNK), S_CHUNK ** -0.5),
        'b_b_s': 1.0 + nrm((N_B, S_GROUPS, S_CHUNK), 0.1),
        'w_b_out': nrm((N_B, S_WIDTH, D_MODEL), BETA * S_WIDTH ** -0.5),
        'ln_g': 1.0 + nrm((DEPTH, 2, D_MODEL), 0.02),
        'ln_b': nrm((DEPTH, 2, D_MODEL), 0.02),
        'peer_wq': nrm((DEPTH, D_MODEL, P_HEADS * P_DQ), D_MODEL ** -0.5),
        'peer_keys': nrm((DEPTH, 2, P_NKEYS, P_DH), P_DH ** -0.5),
        'peer_u': nrm((DEPTH, P_NEXP, D_MODEL), D_MODEL ** -0.5),
        'peer_v': nrm((DEPTH, P_NEXP, D_MODEL), BETA * P_HEADS ** -0.5),
        'ple_w': nrm((DEPTH, PLE_DIM, D_MODEL), PLE_DIM ** -0.5),
        'ple_gate_w': nrm((DEPTH, D_MODEL, D_MODEL), D_MODEL ** -0.5),
    }


def reference(x_prompt, x_sample, state_C, state_n, state_m, p_prompt, p_sample, w_a_in, b_a_gate, a_norm_w, w_a_out, w_b_in, b_b_in, b_norm_w, w_b_s, b_b_s, w_b_out, ln_g, ln_b, peer_wq, peer_keys, peer_u, peer_v, ple_w, ple_gate_w):
    B = x_prompt.shape[0]
    C0 = jnp.zeros((N_A, B, M_HEADS, M_DK, M_DV), jnp.float32)
    n0 = jnp.zeros((N_A, B, M_HEADS, M_DK), jnp.float32)
    m0 = jnp.zeros((N_A, B, M_HEADS), jnp.float32)
    y_prompt, prompt_C, prompt_n, prompt_m, _ = trunk(
        x_prompt, p_prompt, C0, n0, m0, w_a_in, b_a_gate, a_norm_w, w_a_out, w_b_in, b_b_in, b_norm_w, w_b_s, b_b_s, w_b_out,
        ln_g, ln_b, peer_wq, peer_keys, peer_u, peer_v, ple_w, ple_gate_w)
    y_sample, sample_C, sample_n, sample_m, sample_v = trunk(
        x_sample, p_sample, state_C, state_n, state_m, w_a_in, b_a_gate, a_norm_w, w_a_out, w_b_in, b_b_in, b_norm_w, w_b_s, b_b_s, w_b_out,
        ln_g, ln_b, peer_wq, peer_keys, peer_u, peer_v, ple_w, ple_gate_w)
    return (y_prompt, y_sample, prompt_C, prompt_n, prompt_m, sample_C, sample_n, sample_m, sample_v)
```

```python
from contextlib import ExitStack
import numpy as np
import concourse.bass as bass
import concourse.mybir as mybir
from concourse.bass_utils import run_bass_kernel_spmd

F32 = mybir.dt.float32
BF16 = mybir.dt.bfloat16
U32 = mybir.dt.uint32
I32 = mybir.dt.int32
ALU = mybir.AluOpType
AF = mybir.ActivationFunctionType
AX = mybir.AxisListType

D = 2048
KC = 16
HEADS = 8
DK = 128
DV = 256
AIN = 2 * HEADS * DK + 2 * HEADS * DV + 2 * HEADS
NEXP = 16384
PLE = 256
LN_EPS = 1e-5
NEG = -1e30


class Buf:
    __slots__ = ("name", "w", "r", "dsem", "dcnt")

    def __init__(self, name):
        self.name = name
        self.w = None
        self.r = {}
        self.dsem = None
        self.dcnt = 0


class Sync:
    ROT = 30000

    def __init__(self, nc):
        self.nc = nc
        self.eng = {"pe": nc.tensor, "act": nc.scalar, "dve": nc.vector, "pool": nc.gpsimd, "sp": nc.sync}
        self.sems = {}
        self.ecur = {}
        self.seen = {e: {} for e in self.eng}
        self.nsem = 0
        self.ninst = {e: 0 for e in self.eng}
        self.free_d = []
        self.dbufs = []
        self.retired = []
        self.pool_fifo = []
        self.last_big = None
        self.uid = 0
        for e in self.eng:
            self._new_eng_sem(e)

    def _alloc(self, key):
        h = self.nc.alloc_semaphore(name="s%d" % self.nsem)
        self.nsem += 1
        self.sems[key] = h
        return key

    def _new_eng_sem(self, e):
        ep = 0 if e not in self.ecur else self.ecur[e][0][2] + 1
        key = ("e", e, ep)
        self._alloc(key)
        self.ecur[e] = (key, 0)

    def _wait(self, eng, deps):
        best = {}
        for (k, c) in deps:
            if k[0] == "e" and k[1] == "pe" and eng == "pe":
                continue
            if self.seen[eng].get(k, 0) >= c:
                continue
            if best.get(k, 0) < c:
                best[k] = c
        for k, c in best.items():
            self.eng[eng].wait_ge(self.sems[k], c)
            self.seen[eng][k] = c
            self.ninst[eng] += 1

    @staticmethod
    def _deps(reads, writes):
        deps = []
        for b in reads:
            if b.w is not None:
                deps.append(b.w)
        for b in writes:
            if b.w is not None:
                deps.append(b.w)
            deps.extend(b.r.items())
        return deps

    @staticmethod
    def _record(rec, reads, writes):
        k, c = rec
        for b in reads:
            if b.r.get(k, 0) < c:
                b.r[k] = c
        for b in writes:
            b.w = rec
            b.r = {}

    def op(self, eng, fn, reads=(), writes=()):
        self._wait(eng, self._deps(reads, writes))
        inst = fn()
        key, cnt = self.ecur[eng]
        cnt += 1
        inst.then_inc(self.sems[key], 1)
        self.ecur[eng] = (key, cnt)
        self._record((key, cnt), reads, writes)
        self.ninst[eng] += 1
        if cnt >= self.ROT:
            self._new_eng_sem(eng)
        return inst

    def dma(self, q, fn, reads=(), writes=(), owner=None, big=False):
        if owner is None:
            owner = writes[0] if writes else reads[0]
        extra = []
        if q == "pool":
            while len(self.pool_fifo) >= 4:
                extra.append(self.pool_fifo.pop(0))
            if big and self.last_big is not None:
                extra.append(self.last_big)
        self._wait(q, self._deps(reads, writes) + extra)
        if owner.dsem is not None and owner.dcnt >= self.ROT:
            self.retired.append((owner.dsem, owner.dcnt))
            owner.dsem = None
        if owner.dsem is None:
            if self.free_d:
                owner.dsem, owner.dcnt = self.free_d.pop()
            else:
                self.uid += 1
                owner.dsem = self._alloc(("d", self.uid, 0))
                owner.dcnt = 0
            self.dbufs.append(owner)
        inst = fn()
        owner.dcnt += 16
        inst.then_inc(self.sems[owner.dsem], 16)
        self._record((owner.dsem, owner.dcnt), reads, writes)
        if q == "pool":
            self.pool_fifo.append((owner.dsem, owner.dcnt))
            if big:
                self.last_big = (owner.dsem, owner.dcnt)
        self.ninst[q] += 1
        return inst

    def wait_all(self, eng, bufs):
        deps = []
        for b in bufs:
            if b.w is not None:
                deps.append(b.w)
            deps.extend(b.r.items())
        self._wait(eng, deps)

    def barrier(self):
        deps = [self.ecur[e] for e in self.eng if self.ecur[e][1] > 0]
        for k in list(self.sems):
            if k[0] == "e" and k != self.ecur[k[1]][0]:
                deps.append((k, self.ROT))
        for b in self.dbufs:
            if b.dsem is not None and b.dcnt > 0:
                deps.append((b.dsem, b.dcnt))
        deps.extend(self.retired)
        self.retired = []
        self.pool_fifo = []
        self.last_big = None
        for e in self.eng:
            self._wait(e, deps)
        for b in self.dbufs:
            if b.dsem is not None:
                if b.dcnt < self.ROT:
                    self.free_d.append((b.dsem, b.dcnt))
                b.dsem = None
        self.dbufs = []


def build_program(cfg):
    DEPTH = cfg["DEPTH"]
    SEQ = cfg["SEQ"]
    BT = cfg["BT"]
    NSQ = cfg["NSQ"]
    NS = NSQ * 4
    NBLK = SEQ // (BT * 128)
    NA = (DEPTH + 1) // 2
    NB = DEPTH // 2
    ALPHA = float((2 * DEPTH) ** 0.25)
    TB = BT * 128 + NS
    NCHP = BT * 2
    assert NS <= 64 and SEQ % (BT * 128) == 0

    nc = bass.Bass("TRN2", target_bir_lowering=False)
    S = Sync(nc)
    V, A, P, PE = nc.vector, nc.scalar, nc.gpsimd, nc.tensor

    def din(name, shape, dt=F32):
        return nc.dram_tensor(name, list(shape), dt, kind="ExternalInput").ap()

    def dout(name, shape):
        return nc.dram_tensor(name, list(shape), F32, kind="ExternalOutput").ap()

    xp = din("xp", [SEQ, D]); xs = din("xs", [NS, D])
    sC = din("sC", [NA, NSQ, HEADS, DK, DV]); sn = din("sn", [NA, NSQ, HEADS, DK]); sm = din("sm", [NA, NSQ, HEADS])
    pp = din("pp", [DEPTH, SEQ, PLE]); pps = din("pps", [DEPTH, NS, PLE])
    w_a_in = din("w_a_in", [NA, D, AIN]); b_a_gate = din("b_a_gate", [NA, 16]); a_norm_w = din("a_norm_w", [NA, D])
    w_a_out = din("w_a_out", [NA, D, D])
    w_b_in = din("w_b_in", [NB, D, 2 * D]); b_b_in = din("b_b_in", [NB, 2 * D]); b_norm_w = din("b_norm_w", [NB, D])
    w_b_s = din("w_b_s", [NB, 8, 128, 128]); b_b_s = din("b_b_s", [NB, 8, 128]); w_b_out = din("w_b_out", [NB, D, D])
    ln_g = din("ln_g", [DEPTH, 2, D]); ln_b = din("ln_b", [DEPTH, 2, D])
    peer_wq = din("peer_wq", [DEPTH, D, D]); peer_keys = din("peer_keys", [DEPTH, 2, 128, 128])
    peer_u = din("peer_u", [DEPTH, NEXP, D]); peer_v = din("peer_v", [DEPTH, NEXP, D])
    peer_u_flat = peer_u.rearrange("l e d -> (l e) d"); peer_v_flat = peer_v.rearrange("l e d -> (l e) d")
    ple_w = din("ple_w", [DEPTH, PLE, D]); ple_gate_w = din("ple_gate_w", [DEPTH, D, D])

    yp = dout("yp", [SEQ, D]); ys = dout("ys", [NS, D])
    pC = dout("pC", [NA, HEADS, DK, DV]); pn = dout("pn", [NA, HEADS, DK]); pm = dout("pm", [NA, HEADS])
    sCo = dout("sCo", [NA, NSQ, HEADS, DK, DV]); sno = dout("sno", [NA, NSQ, HEADS, DK]); smo = dout("smo", [NA, NSQ, HEADS])
    svo = dout("svo", [NB, NS, D])

    out_bufs = []

    top = ExitStack()

    sbuid = [0]

    def sb(es, name, shape, dt=F32):
        sbuid[0] += 1
        return es.enter_context(nc.sbuf_tensor("%s_%d" % (name, sbuid[0]), list(shape), dt))

    def dr(ap2d_tensor, offset, pat):
        return bass.AP(ap2d_tensor, offset, pat)

    def row_bcast(ap_row, nparts):
        n = ap_row.shape[-1]
        return bass.AP(ap_row.tensor, ap_row.offset, [[0, nparts], [1, n]])

    ident = sb(top, "ident", [128, 128]); Bconst = Buf("const")
    U128 = sb(top, "U128", [128, 128])
    maskS = sb(top, "maskS", [64, 64])
    rowsel = sb(top, "rowsel", [64, NSQ])
    segsel = sb(top, "segsel", [128, NSQ, NS], BF16)
    iota16 = sb(top, "iota16", [128, 16])
    RepT = sb(top, "RepT", [4, 64])
    ones8 = sb(top, "ones8", [8, 128])
    onesr = sb(top, "onesr", [1, 128])
    onesg = sb(top, "onesg", [8, BT * 128 + 64])
    X = sb(top, "X", [128, BT + 1, D]); BX = [Buf("X%d" % i) for i in range(BT + 1)]
    XT = sb(top, "XT", [128, KC, TB], BF16); BXT = [Buf("XT%d" % i) for i in range(BT + 1)]
    CA = sb(top, "CA", [128, NA, HEADS, 257]); BCA = [[Buf("CA%d_%d" % (j, h)) for h in range(HEADS)] for j in range(NA)]
    mst = sb(top, "mst", [8, NA]); Bmst = [Buf("mst%d" % j) for j in range(NA)]
    psum = [top.enter_context(nc.psum_tensor("ps%d" % i, [128, 512], F32)) for i in range(8)]
    Bps = [Buf("ps%d" % i) for i in range(8)]
    ps_rr = [0]

    def next_ps():
        i = ps_rr[0]
        ps_rr[0] = (i + 1) % 7
        return psum[i], Bps[i]

    with ExitStack() as es:
        it = sb(es, "iot", [128, 2048], I32); Bit = Buf("iot")
        tmpf = sb(es, "tmpf", [128, 2048]); Btf = Buf("tmpf")

        def aff(out_ap, n_part, pattern, cm, op, thr, base=0):
            nfree = 1
            for st_, n_ in pattern:
                nfree *= n_
            S.op("pool", lambda: P.iota(it[0:n_part, 0:nfree], pattern=pattern, base=base, channel_multiplier=cm),
                 writes=[Bit])
            S.op("dve", lambda: V.tensor_single_scalar(out=out_ap, in_=it[0:n_part, 0:nfree], scalar=thr, op=op),
                 reads=[Bit], writes=[Bconst])

        aff(ident[:], 128, [[1, 128]], -1, ALU.is_equal, 0)
        aff(U128[:], 128, [[1, 128]], -1, ALU.is_ge, 0)
        aff(rowsel[:], 64, [[-4, NSQ]], 1, ALU.is_ge, 0)
        aff(tmpf[0:64, 0:NSQ], 64, [[-4, NSQ]], 1, ALU.is_le, 3)
        S.op("dve", lambda: V.tensor_tensor(out=rowsel[:], in0=rowsel[:], in1=tmpf[0:64, 0:NSQ], op=ALU.mult),
             reads=[Bconst], writes=[Bconst])
        S.op("dve", lambda: V.memset(maskS[:], 0.0), writes=[Bconst])
        S.op("dve", lambda: V.tensor_tensor(
            out=maskS[:, 0:NS].rearrange("p (b t) -> p b t", t=4),
            in0=U128[0:64, 0:NS].rearrange("p (b t) -> p b t", t=4),
            in1=rowsel[:, :, None].to_broadcast([64, NSQ, 4]), op=ALU.mult), reads=[Bconst], writes=[Bconst])
        aff(tmpf[:, 0:NSQ * NS], 128, [[-4, NSQ], [1, NS]], 0, ALU.is_ge, 0)
        aff(tmpf[:, 1024:1024 + NSQ * NS], 128, [[-4, NSQ], [1, NS]], 0, ALU.is_le, 3)
        S.op("dve", lambda: V.tensor_tensor(out=segsel[:].rearrange("p a l -> p (a l)"), in0=tmpf[:, 0:NSQ * NS],
                                            in1=tmpf[:, 1024:1024 + NSQ * NS], op=ALU.mult),
             reads=[Bconst], writes=[Bconst])
        aff(iota16[:], 128, [[1, 16]], 0, ALU.add, 0)
        S.op("pool", lambda: P.iota(it[0:4, 0:64], pattern=[[1, 64]], base=4, channel_multiplier=-1), writes=[Bit])
        S.op("dve", lambda: V.tensor_single_scalar(out=it[0:4, 64:128], in_=it[0:4, 0:64], scalar=3, op=ALU.bitwise_and), reads=[Bit], writes=[Bit])
        S.op("dve", lambda: V.tensor_single_scalar(out=RepT[:, :], in_=it[0:4, 64:128], scalar=0, op=ALU.is_equal), reads=[Bit], writes=[Bconst])
        S.op("dve", lambda: V.memset(ones8[:], 1.0), writes=[Bconst])
        S.op("dve", lambda: V.memset(onesr[:], 1.0), writes=[Bconst])
        S.op("dve", lambda: V.memset(onesg[:], 1.0), writes=[Bconst])
        for j in range(NA):
            for h in range(HEADS):
                S.op("pool", lambda: P.memset(CA[:, j, h, :], 0.0), writes=[BCA[j][h]])
            S.op("dve", lambda: V.memset(mst[:, j:j + 1], 0.0), writes=[Bmst[j]])
        S.barrier()

    evac_rr = [0]

    def evac_copy(out_ap, in_ap, reads, writes):
        evac_rr[0] ^= 1
        if evac_rr[0]:
            S.op("act", lambda: A.copy(out=out_ap, in_=in_ap), reads=reads, writes=writes)
        else:
            S.op("dve", lambda: V.tensor_copy(out=out_ap, in_=in_ap), reads=reads, writes=writes)

    def transpose_tile(src_ap_fn, rows, dst, dst_col0, Bsrc, Bdst, nkc=KC):
        for kg in range(0, nkc, 4):
            ps, bp = next_ps()
            nq = min(4, nkc - kg)
            for q in range(nq):
                kc = kg + q
                S.op("pe", lambda: PE.transpose(ps[:, q * 128:q * 128 + rows], src_ap_fn(kc), ident[0:rows, 0:rows]),
                     reads=[Bsrc, Bconst], writes=[bp])
            evac_copy(dst[:, kg:kg + nq, dst_col0:dst_col0 + rows],
                      ps[:, 0:nq * 128].rearrange("p (q t) -> p q t", t=128)[:, :, 0:rows], [bp], [Bdst])

    def load_panel(wslot, Bw, dram2d, c0, ncols, nkc=KC, coff=0):
        src = dram2d[:, c0:c0 + ncols].rearrange("(kc p) n -> p kc n", p=128)
        S.dma("pool", lambda: P.dma_start(out=wslot[:, 0:nkc, coff:coff + ncols], in_=src), writes=[Bw], big=True)

    def layer_norm_tile(i, rows, gt, bt, Bgb, wk):
        st, mv, rs, Bw = wk
        for q in range(4):
            S.op("dve", lambda: V.bn_stats(out=st[0:rows, q, :], in_=X[0:rows, i, q * 512:(q + 1) * 512]),
                 reads=[BX[i]], writes=[Bw])
        S.op("dve", lambda: V.bn_aggr(out=mv[0:rows, :], in_=st[0:rows, :, :].rearrange("p a b -> p (a b)")),
             reads=[Bw], writes=[Bw])
        S.op("act", lambda: A.activation(out=rs[0:rows, :], in_=mv[0:rows, 1:2], func=AF.Sqrt, bias=LN_EPS, scale=1.0),
             reads=[Bw], writes=[Bw])
        S.op("dve", lambda: V.reciprocal(out=rs[0:rows, :], in_=rs[0:rows, :]), reads=[Bw], writes=[Bw])
        S.op("dve", lambda: V.tensor_scalar(out=X[0:rows, i, :], in0=X[0:rows, i, :], scalar1=mv[0:rows, 0:1],
                                            scalar2=rs[0:rows, 0:1], op0=ALU.subtract, op1=ALU.mult),
             reads=[BX[i], Bw], writes=[BX[i]])
        S.op("pool", lambda: P.tensor_tensor(out=X[0:rows, i, :], in0=X[0:rows, i, :], in1=gt[0:rows, :], op=ALU.mult),
             reads=[BX[i], Bgb], writes=[BX[i]])
        S.op("pool", lambda: P.tensor_tensor(out=X[0:rows, i, :], in0=X[0:rows, i, :], in1=bt[0:rows, :], op=ALU.add),
             reads=[BX[i], Bgb], writes=[BX[i]])

    def out_proj_and_ln1(tiles, HT, BHT, w2d, li, wp, Bwp, wrr):
        with ExitStack() as es:
            gt = sb(es, "ln_gt", [128, D]); bt = sb(es, "ln_bt", [128, D]); Bgb = Buf("lngb")
            st = sb(es, "ln_st", [128, 4, 6]); mv = sb(es, "ln_mv", [128, 2]); rs = sb(es, "ln_rs", [128, 1]); Bw = Buf("lnw")
            S.dma("sp", lambda: nc.sync.dma_start(out=gt[:], in_=row_bcast(ln_g[li, 0], 128)), writes=[Bgb])
            S.dma("act", lambda: A.dma_start(out=bt[:], in_=row_bcast(ln_b[li, 0], 128)), writes=[Bgb])
            for cp in range(D // 256):
                s_ = wrr[0] % len(wp); wrr[0] += 1
                load_panel(wp[s_], Bwp[s_], w2d, cp * 256, 256)
                for t in tiles:
                    i, rows, c0 = t["i"], t["rows"], t["c0"]
                    ps, bp = next_ps()
                    for kc in range(KC):
                        S.op("pe", lambda: PE.matmul(ps[0:rows, 0:256], lhsT=HT[:, kc, c0:c0 + rows], rhs=wp[s_][:, kc, 0:256],
                                                     start=(kc == 0), stop=(kc == KC - 1)),
                             reads=[BHT, Bwp[s_]], writes=[bp])
                    S.op("dve", lambda: V.scalar_tensor_tensor(out=X[0:rows, i, cp * 256:(cp + 1) * 256],
                                                               in0=X[0:rows, i, cp * 256:(cp + 1) * 256], scalar=ALPHA,
                                                               in1=ps[0:rows, 0:256], op0=ALU.mult, op1=ALU.add),
                         reads=[BX[i], bp], writes=[BX[i]])
            for t in tiles:
                layer_norm_tile(t["i"], t["rows"], gt, bt, Bgb, (st, mv, rs, Bw))
            S.barrier()

    def build_XT(tiles):
        for t in tiles:
            i, rows, c0 = t["i"], t["rows"], t["c0"]
            transpose_tile(lambda kc: X[0:rows, i, kc * 128:(kc + 1) * 128], rows, XT, c0, BX[i], BXT[i])

    def mlstm_mixer(blk, li, j, tiles, chunks, last):
        T = tiles[-1]["c0"] + tiles[-1]["rows"]
        PT = BT * 128
        nch = len(chunks)
        with ExitStack() as es:
            HT = sb(es, "HT", [128, KC, TB], BF16); BHT = Buf("HT")
            wp = [sb(es, "wp%d" % k, [128, KC, 256], BF16) for k in range(4)]
            Bwp = [Buf("wp%d" % k) for k in range(4)]
            wrr = [0]
            with ExitStack() as es2:
                g1 = sb(es2, "g1", [8, TB]); g2 = sb(es2, "g2", [8, TB]); g3 = sb(es2, "g3", [8, TB]); g4 = sb(es2, "g4", [8, TB])
                Bg = Buf("gates")
                wg = sb(es2, "wg", [128, KC, 16], BF16); Bwg = Buf("wg")
                bg = sb(es2, "bg", [8, 2]); nbf = sb(es2, "nbf", [8, 1])
                Bst = sb(es2, "Bst", [8, 64]); av = sb(es2, "av", [8, 64]); bLt = sb(es2, "bLt", [8, 64]); mrun = sb(es2, "mrun", [8, 64])
                m0s = sb(es2, "m0s", [8, NSQ]); mnew = sb(es2, "mnew", [8, NSQ]); nmn = sb(es2, "nmn", [8, NSQ])
                Dg = sb(es2, "Dg", [8, 128]); Bdg = Buf("Dg")
                E = sb(es2, "E", [64, nch, 24]); BE = Buf("E")
                Fb = sb(es2, "Fb", [128, NCHP * 8]); FbS = sb(es2, "FbS", [128, NSQ * 8]); em0 = sb(es2, "em0", [128, NSQ * 8])
                enm = sb(es2, "enm", [128, NSQ * 8]); enmP = sb(es2, "enmP", [128, 8]); BF = Buf("Fb")
                mcp = sb(es2, "mcp", [8, 1]); Bmo = Buf("mout"); Bmo2 = Buf("mout2")
                load_panel(wg, Bwg, w_a_in[j], 2 * HEADS * DK + 2 * HEADS * DV, 16)
                S.dma("sp", lambda: nc.sync.dma_start(out=bg[:], in_=b_a_gate[j].rearrange("(c h) -> h c", h=8), allow_slow_non_contiguous=True), writes=[Bg])
                S.op("dve", lambda: V.tensor_scalar(out=nbf[:], in0=bg[:, 1:2], scalar1=-1.0, scalar2=None, op0=ALU.mult),
                     reads=[Bg], writes=[Bg])
                groups = [(t0, min(512, T - t0)) for t0 in range(0, T, 512)]
                for (t0, n) in groups:
                    for which in range(2):
                        ps, bp = next_ps()
                        for kc in range(KC):
                            S.op("pe", lambda: PE.matmul(ps[0:8, 0:n], lhsT=wg[:, kc, which * 8:(which + 1) * 8],
                                                         rhs=XT[:, kc, t0:t0 + n], start=(kc == 0), stop=(kc == KC - 1)),
                                 reads=[Bwg] + BXT, writes=[bp])
                        if which == 0:
                            S.op("act", lambda: A.activation(out=g1[:, t0:t0 + n], in_=ps[0:8, 0:n], func=AF.Identity,
                                                             bias=bg[:, 0:1], scale=1.0), reads=[bp, Bg], writes=[Bg])
                        else:
                            S.op("act", lambda: A.activation(out=g2[:, t0:t0 + n], in_=ps[0:8, 0:n], func=AF.Exp,
                                                             bias=nbf[:, 0:1], scale=-1.0), reads=[bp, Bg], writes=[Bg])
                S.op("act", lambda: A.activation(out=g2[:, 0:T], in_=g2[:, 0:T], func=AF.Ln, bias=1.0, scale=1.0),
                     reads=[Bg], writes=[Bg])
                S.op("dve", lambda: V.tensor_scalar(out=g2[:, 0:T], in0=g2[:, 0:T], scalar1=-1.0, scalar2=None, op0=ALU.mult),
                     reads=[Bg], writes=[Bg])
                S.op("dve", lambda: V.tensor_tensor_scan(out=g3[:, 0:PT], data0=onesg[:, 0:PT], data1=g2[:, 0:PT], initial=0.0,
                                                         op0=ALU.mult, op1=ALU.add), reads=[Bg, Bconst], writes=[Bg])
                if last:
                    S.op("dve", lambda: V.tensor_tensor_scan(out=g3[:, PT:T], data0=onesg[:, 0:NS], data1=g2[:, PT:T], initial=0.0,
                                                             op0=ALU.mult, op1=ALU.add), reads=[Bg, Bconst], writes=[Bg])
                parts = [(0, PT, 64, NCHP, 0)]
                if last:
                    parts.append((PT, T, 4, NSQ, NCHP))
                for (a0, a1, L, ncn, co) in parts:
                    B3 = g3[:, a0:a1].rearrange("p (c l) -> p c l", l=L)
                    b3 = g2[:, a0:a1].rearrange("p (c l) -> p c l", l=L)
                    ig3 = g1[:, a0:a1].rearrange("p (c l) -> p c l", l=L)
                    d3 = g4[:, a0:a1].rearrange("p (c l) -> p c l", l=L)
                    S.op("dve", lambda: V.memset(Bst[:, co:co + 1], 0.0), writes=[Bg])
                    if ncn > 1:
                        S.op("dve", lambda: V.tensor_copy(out=Bst[:, co + 1:co + ncn], in_=B3[:, 0:ncn - 1, L - 1]),
                             reads=[Bg], writes=[Bg])
                    S.op("dve", lambda: V.tensor_tensor(out=b3, in0=B3, in1=Bst[:, co:co + ncn, None].to_broadcast([8, ncn, L]),
                                                        op=ALU.subtract), reads=[Bg], writes=[Bg])
                    S.op("dve", lambda: V.tensor_copy(out=bLt[:, co:co + ncn], in_=b3[:, :, L - 1]), reads=[Bg], writes=[Bg])
                    S.op("dve", lambda: V.tensor_tensor(out=d3, in0=ig3, in1=b3, op=ALU.subtract), reads=[Bg], writes=[Bg])
                    S.op("dve", lambda: V.tensor_reduce(out=av[:, co:co + ncn], in_=d3, axis=AX.X, op=ALU.max),
                         reads=[Bg], writes=[Bg])
                    S.op("dve", lambda: V.tensor_tensor(out=ig3, in0=d3, in1=bLt[:, co:co + ncn, None].to_broadcast([8, ncn, L]),
                                                        op=ALU.add), reads=[Bg], writes=[Bg])
                S.op("dve", lambda: V.tensor_tensor_scan(out=mrun[:, 0:NCHP], data0=av[:, 0:NCHP], data1=bLt[:, 0:NCHP],
                                                         initial=mst[:, j:j + 1], op0=ALU.max, op1=ALU.add),
                     reads=[Bg, Bmst[j]], writes=[Bg])
                S.op("dve", lambda: V.tensor_copy(out=mst[:, j:j + 1], in_=mrun[:, NCHP - 1:NCHP]), reads=[Bg], writes=[Bmst[j]])

                def bcast_exp(dst, src8, ncol, scale):
                    S.op("dve", lambda: V.tensor_tensor(out=Dg[:, 0:ncol * 8].rearrange("p (c h) -> p c h", h=8),
                                                        in0=src8[:, :, None].to_broadcast([8, ncol, 8]),
                                                        in1=ident[0:8, None, 0:8].to_broadcast([8, ncol, 8]), op=ALU.mult),
                         reads=[Bg, Bconst, Bmst[j]], writes=[Bdg])
                    ps, bp = next_ps()
                    S.op("pe", lambda: PE.matmul(ps[:, 0:ncol * 8], lhsT=ones8[:, :], rhs=Dg[:, 0:ncol * 8], start=True, stop=True),
                         reads=[Bdg, Bconst], writes=[bp])
                    S.op("act", lambda: A.activation(out=dst, in_=ps[:, 0:ncol * 8], func=AF.Exp, scale=scale), reads=[bp], writes=[BF])

                bcast_exp(Fb[:, :], bLt[:, 0:NCHP], NCHP, 1.0)
                if last:
                    S.dma("sp", lambda: nc.sync.dma_start(out=m0s[:], in_=sm[j].rearrange("s h -> h s"), allow_slow_non_contiguous=True), writes=[Bg])
                    S.op("dve", lambda: V.tensor_tensor(out=mnew[:], in0=m0s[:], in1=av[:, NCHP:NCHP + NSQ], op=ALU.max),
                         reads=[Bg], writes=[Bg])
                    S.op("dve", lambda: V.tensor_tensor(out=mnew[:], in0=mnew[:], in1=bLt[:, NCHP:NCHP + NSQ], op=ALU.add),
                         reads=[Bg], writes=[Bg])
                    bcast_exp(FbS[:, :], bLt[:, NCHP:NCHP + NSQ], NSQ, 1.0)
                    bcast_exp(em0[:, :], m0s[:, :], NSQ, 1.0)
                    bcast_exp(enm[:, :], mnew[:, :], NSQ, -1.0)
                    bcast_exp(enmP[:, :], mst[:, j:j + 1], 1, -1.0)
                    S.dma("sp", lambda: nc.sync.dma_start(out=smo[j].rearrange("s h -> h s"), in_=mnew[:], allow_slow_non_contiguous=True), reads=[Bg], owner=Bmo2)
                    S.op("dve", lambda: V.tensor_copy(out=mcp[:], in_=mst[:, j:j + 1]), reads=[Bmst[j]], writes=[Bmo])
                    S.dma("sp", lambda: nc.sync.dma_start(out=pm[j].rearrange("(h o) -> h o", o=1), in_=mcp[:]), reads=[Bmo], owner=Bmo)
                    out_bufs.append(Bmo)
                for ci, ch in enumerate(chunks):
                    c0, n = ch["c0"], ch["n"]
                    ps, bp = next_ps()
                    for k, src in enumerate((g4, g2, g1)):
                        S.op("pe", lambda: PE.transpose(ps[0:n, k * 8:(k + 1) * 8], src[0:8, c0:c0 + n], ident[0:8, 0:8]),
                             reads=[Bg, Bconst], writes=[bp])
                    S.op("act", lambda: A.activation(out=E[0:n, ci, :], in_=ps[0:n, 0:24], func=AF.Exp), reads=[bp], writes=[BE])

                qT = sb(es2, "qT", [128, TB], BF16); kT = sb(es2, "kT", [128, TB], BF16); BqT = Buf("qT"); BkT = Buf("kT")
                nwb = sb(es2, "nwb", [64, 2, 256]); Bnw = [Buf("nw0"), Buf("nw1")]
                NBUF = 2
                kk = [sb(es2, "kk%d" % k, [64, 128], BF16) for k in range(NBUF)]
                v1 = [sb(es2, "v1_%d" % k, [64, 257], BF16) for k in range(NBUF)]
                v2 = [sb(es2, "v2_%d" % k, [64, 257], BF16) for k in range(NBUF)]
                so = [sb(es2, "so%d" % k, [64, 256]) for k in range(NBUF)]
                SM = [sb(es2, "SM%d" % k, [64, 64], BF16) for k in range(NBUF)]
                hh = [sb(es2, "hh%d" % k, [64, 256]) for k in range(NBUF)]
                sm8 = [sb(es2, "sm8_%d" % k, [64, 16]) for k in range(NBUF)]
                Bck = [Buf("ck%d" % k) for k in range(NBUF)]
                Bhk = [Buf("hk%d" % k) for k in range(NBUF)]
                Cb = sb(es2, "Cb", [128, 257], BF16); BCb = Buf("Cb")
                if last:
                    qTm = sb(es2, "qTm", [128, NSQ, NS], BF16); Bqm = Buf("qTm")
                    HS = max(1, NSQ // 2)
                    Cs = sb(es2, "Cs", [128, HS, 257]); Csb = sb(es2, "Csb", [128, HS, 257], BF16); BCs = Buf("Cs"); BCsb = Buf("Csb")
                    v2m = sb(es2, "v2m", [64, NSQ, 257], BF16); Bv2m = Buf("v2m")
                    cst = [sb(es2, "cst%d" % k, [128, 257]) for k in range(2)]; Bcst = [Buf("cst0"), Buf("cst1")]
                    pco = sb(es2, "pco", [128, 257]); Bpco = Buf("pco")
                    out_bufs.extend(Bcst); out_bufs.append(Bpco)
                cstr = [0]
                slot_rr = [0]
                wa = w_a_in[j]
                for h in range(HEADS):
                    sqk = wrr[0] % 4; wrr[0] += 1
                    load_panel(wp[sqk], Bwp[sqk], wa, h * DK, DK, coff=0)
                    load_panel(wp[sqk], Bwp[sqk], wa, HEADS * DK + h * DK, DK, coff=128)
                    sv = wrr[0] % 4; wrr[0] += 1
                    load_panel(wp[sv], Bwp[sv], wa, 2 * HEADS * DK + h * DV, DV)
                    so_ = wrr[0] % 4; wrr[0] += 1
                    load_panel(wp[so_], Bwp[so_], wa, 2 * HEADS * DK + HEADS * DV + h * DV, DV)
                    S.dma("act", lambda: A.dma_start(out=nwb[:, h % 2, :], in_=row_bcast(a_norm_w[j, h * DV:(h + 1) * DV], 64)),
                          writes=[Bnw[h % 2]])
                    for (t0, n) in groups:
                        for which, dstT, Bd in ((0, qT, BqT), (1, kT, BkT)):
                            ps, bp = next_ps()
                            for kc in range(KC):
                                S.op("pe", lambda: PE.matmul(ps[:, 0:n], lhsT=wp[sqk][:, kc, which * 128:(which + 1) * 128],
                                                             rhs=XT[:, kc, t0:t0 + n], start=(kc == 0), stop=(kc == KC - 1)),
                                     reads=[Bwp[sqk]] + BXT, writes=[bp])
                            if which == 0:
                                S.op("act", lambda: A.mul(dstT[:, t0:t0 + n], ps[:, 0:n], float(DK ** -0.5)), reads=[bp], writes=[Bd])
                            else:
                                S.op("dve", lambda: V.tensor_copy(out=dstT[:, t0:t0 + n], in_=ps[:, 0:n]), reads=[bp], writes=[Bd])
                    S.op("act", lambda: A.copy(out=Cb[:], in_=CA[:, j, h, :]), reads=[BCA[j][h]], writes=[BCb])
                    for ci, ch in enumerate(chunks):
                        c0, n, kind = ch["c0"], ch["n"], ch["kind"]
                        k = slot_rr[0] % NBUF; slot_rr[0] += 1
                        ps, bp = next_ps()
                        for kc in range(KC):
                            S.op("pe", lambda: PE.matmul(ps[0:n, 0:128], lhsT=XT[:, kc, c0:c0 + n], rhs=wp[sqk][:, kc, 128:256],
                                                         start=(kc == 0), stop=(kc == KC - 1)), reads=[Bwp[sqk]] + BXT, writes=[bp])
                        S.op("act", lambda: A.copy(out=kk[k][0:n, :], in_=ps[0:n, 0:128]), reads=[bp], writes=[Bck[k]])
                        ps, bp = next_ps()
                        for kc in range(KC):
                            S.op("pe", lambda: PE.matmul(ps[0:n, 0:256], lhsT=XT[:, kc, c0:c0 + n], rhs=wp[sv][:, kc, 0:256],
                                                         start=(kc == 0), stop=(kc == KC - 1)), reads=[Bwp[sv]] + BXT, writes=[bp])
                        S.op("dve", lambda: V.tensor_scalar(out=v1[k][0:n, 0:256], in0=ps[0:n, 0:256], scalar1=E[0:n, ci, h:h + 1],
                                                            scalar2=None, op0=ALU.mult), reads=[bp, BE], writes=[Bck[k]])
                        S.op("act", lambda: A.activation(out=v2[k][0:n, 0:256], in_=ps[0:n, 0:256], func=AF.Copy,
                                                         scale=E[0:n, ci, 16 + h:17 + h]), reads=[bp, BE], writes=[Bck[k]])
                        S.op("dve", lambda: V.tensor_copy(out=v1[k][0:n, 256:257], in_=E[0:n, ci, h:h + 1]), reads=[BE], writes=[Bck[k]])
                        S.op("dve", lambda: V.tensor_copy(out=v2[k][0:n, 256:257], in_=E[0:n, ci, 16 + h:17 + h]), reads=[BE], writes=[Bck[k]])
                        ps, bp = next_ps()
                        for kc in range(KC):
                            S.op("pe", lambda: PE.matmul(ps[0:n, 0:256], lhsT=XT[:, kc, c0:c0 + n], rhs=wp[so_][:, kc, 0:256],
                                                         start=(kc == 0), stop=(kc == KC - 1)), reads=[Bwp[so_]] + BXT, writes=[bp])
                        S.op("act", lambda: A.activation(out=so[k][0:n, :], in_=ps[0:n, 0:256], func=AF.Sigmoid), reads=[bp], writes=[Bck[k]])
                        ps, bp = next_ps()
                        S.op("pe", lambda: PE.matmul(ps[0:n, 0:n], lhsT=kT[:, c0:c0 + n], rhs=qT[:, c0:c0 + n], start=True, stop=True),
                             reads=[BqT, BkT], writes=[bp])
                        msk = U128[0:n, 0:n] if kind == "p" else maskS[0:n, 0:n]
                        S.op("dve", lambda: V.tensor_tensor(out=SM[k][0:n, 0:n], in0=ps[0:n, 0:n], in1=msk, op=ALU.mult),
                             reads=[bp, Bconst], writes=[Bhk[k]])
                        if kind == "p":
                            ps, bp = next_ps()
                        else:
                            ps, bp = psum[7], Bps[7]
                        if kind == "p":
                            S.op("pe", lambda: PE.matmul(ps[0:n, 0:257], lhsT=qT[:, c0:c0 + n], rhs=Cb[:, :], start=True, stop=False),
                                 reads=[BqT, BCb], writes=[bp])
                            S.op("pe", lambda: PE.matmul(ps[0:n, 0:257], lhsT=SM[k][0:n, 0:n], rhs=v1[k][0:n, :], start=False, stop=True),
                                 reads=[Bhk[k], Bck[k]], writes=[bp])
                        else:
                            S.op("dve", lambda: V.tensor_tensor(out=qTm[:], in0=qT[:, None, c0:c0 + n].to_broadcast([128, NSQ, NS]),
                                                                in1=segsel[:], op=ALU.mult), reads=[BqT, Bconst], writes=[Bqm])
                            S.op("dve", lambda: V.tensor_tensor(out=v2m[0:n, :, :], in0=v2[k][0:n, None, :].to_broadcast([n, NSQ, 257]),
                                                                in1=rowsel[0:n, :, None].to_broadcast([n, NSQ, 257]), op=ALU.mult),
                                 reads=[Bck[k], Bconst], writes=[Bv2m])
                            first = True
                            for hs in range(0, NSQ, HS):
                                nh = min(HS, NSQ - hs)
                                S.dma("sp", lambda: nc.sync.dma_start(out=Cs[:, 0:nh, 0:256],
                                                                      in_=sC[j, hs:hs + nh, h].rearrange("s k v -> k s v")), writes=[BCs])
                                S.dma("act", lambda: A.dma_start(out=Cs[:, 0:nh, 256:257],
                                                                 in_=sn[j, hs:hs + nh, h].rearrange("s (k o) -> k s o", o=1), allow_slow_non_contiguous=True), writes=[BCs])
                                emv = em0[:, :].rearrange("p (s g) -> p s g", g=8)[:, hs:hs + nh, h:h + 1]
                                S.op("dve", lambda: V.tensor_tensor(out=Cs[:, 0:nh, :], in0=Cs[:, 0:nh, :],
                                                                    in1=emv.to_broadcast([128, nh, 257]), op=ALU.mult),
                                     reads=[BCs, BF], writes=[BCs])
                                S.op("act", lambda: A.copy(out=Csb[:, 0:nh, :], in_=Cs[:, 0:nh, :]), reads=[BCs], writes=[BCsb])
                                for a in range(nh):
                                    S.op("pe", lambda: PE.matmul(ps[0:n, 0:257], lhsT=qTm[:, hs + a, :], rhs=Csb[:, a, :],
                                                                 start=first, stop=False), reads=[Bqm, BCsb], writes=[bp])
                                    first = False
                                for a in range(nh):
                                    sq = hs + a
                                    ps2, bp2 = next_ps()
                                    S.op("pe", lambda: PE.matmul(ps2[:, 0:257], lhsT=kk[k][0:n, :], rhs=v2m[0:n, sq, :], start=True, stop=True),
                                         reads=[Bck[k], Bv2m], writes=[bp2])
                                    cs_ = cstr[0] % 2; cstr[0] += 1
                                    S.op("dve", lambda: V.scalar_tensor_tensor(out=cst[cs_][:], in0=Cs[:, a, :],
                                                                               scalar=FbS[:, sq * 8 + h:sq * 8 + h + 1], in1=ps2[:, 0:257],
                                                                               op0=ALU.mult, op1=ALU.add), reads=[BCs, BF, bp2], writes=[Bcst[cs_]])
                                    S.op("pool", lambda: P.tensor_scalar(out=cst[cs_][:], in0=cst[cs_][:], scalar1=enm[:, sq * 8 + h:sq * 8 + h + 1],
                                                                         scalar2=None, op0=ALU.mult), reads=[Bcst[cs_], BF], writes=[Bcst[cs_]])
                                    S.dma("sp", lambda: nc.sync.dma_start(out=sCo[j, sq, h], in_=cst[cs_][:, 0:256]), reads=[Bcst[cs_]])
                                    S.dma("act", lambda: A.dma_start(out=sno[j, sq, h].rearrange("(k o) -> k o", o=1), in_=cst[cs_][:, 256:257]),
                                          reads=[Bcst[cs_]])
                            S.op("pe", lambda: PE.matmul(ps[0:n, 0:257], lhsT=SM[k][0:n, 0:n], rhs=v1[k][0:n, :], start=False, stop=True),
                                 reads=[Bhk[k], Bck[k]], writes=[bp])
                        s8 = sm8[k]
                        S.op("dve", lambda: V.tensor_scalar(out=s8[0:n, 3:4], in0=ps[0:n, 256:257], scalar1=E[0:n, ci, 8 + h:9 + h], scalar2=None,
                                                            op0=ALU.mult), reads=[bp, BE], writes=[Bhk[k]])
                        S.op("dve", lambda: V.tensor_scalar(out=s8[0:n, 13:14], in0=s8[0:n, 3:4], scalar1=-1.0, scalar2=1.0,
                                                            op0=ALU.mult, op1=ALU.max), reads=[Bhk[k]], writes=[Bhk[k]])
                        S.op("dve", lambda: V.tensor_tensor(out=s8[0:n, 0:1], in0=s8[0:n, 3:4], in1=s8[0:n, 13:14], op=ALU.max),
                             reads=[Bhk[k]], writes=[Bhk[k]])
                        S.op("dve", lambda: V.reciprocal(out=s8[0:n, 1:2], in_=s8[0:n, 0:1]), reads=[Bhk[k]], writes=[Bhk[k]])
                        S.op("dve", lambda: V.tensor_tensor(out=s8[0:n, 2:3], in0=s8[0:n, 1:2], in1=E[0:n, ci, 8 + h:9 + h], op=ALU.mult),
                             reads=[Bhk[k], BE], writes=[Bhk[k]])
                        S.op("dve", lambda: V.tensor_scalar(out=hh[k][0:n, :], in0=ps[0:n, 0:256], scalar1=s8[0:n, 2:3], scalar2=None, op0=ALU.mult),
                             reads=[bp, Bhk[k]], writes=[Bhk[k]])
                        S.op("dve", lambda: V.bn_stats(out=s8[0:n, 4:10], in_=hh[k][0:n, :]), reads=[Bhk[k]], writes=[Bhk[k]])
                        S.op("dve", lambda: V.bn_aggr(out=s8[0:n, 10:12], in_=s8[0:n, 4:10]), reads=[Bhk[k]], writes=[Bhk[k]])
                        S.op("act", lambda: A.activation(out=s8[0:n, 12:13], in_=s8[0:n, 11:12], func=AF.Sqrt, bias=LN_EPS, scale=1.0),
                             reads=[Bhk[k]], writes=[Bhk[k]])
                        S.op("dve", lambda: V.reciprocal(out=s8[0:n, 12:13], in_=s8[0:n, 12:13]), reads=[Bhk[k]], writes=[Bhk[k]])
                        S.op("dve", lambda: V.tensor_scalar(out=hh[k][0:n, :], in0=hh[k][0:n, :], scalar1=s8[0:n, 10:11], scalar2=s8[0:n, 12:13],
                                                            op0=ALU.subtract, op1=ALU.mult), reads=[Bhk[k]], writes=[Bhk[k]])
                        S.op("pool", lambda: P.tensor_tensor(out=hh[k][0:n, :], in0=hh[k][0:n, :], in1=nwb[0:n, h % 2, :], op=ALU.mult),
                             reads=[Bhk[k], Bnw[h % 2]], writes=[Bhk[k]])
                        S.op("dve", lambda: V.tensor_tensor(out=hh[k][0:n, :], in0=hh[k][0:n, :], in1=so[k][0:n, :], op=ALU.mult),
                             reads=[Bhk[k], Bck[k]], writes=[Bhk[k]])
                        transpose_tile(lambda kc: hh[k][0:n, kc * 128:(kc + 1) * 128], n, HT[:, 2 * h:2 * h + 2, :], c0, Bhk[k], BHT, nkc=2)
                        if kind == "p":
                            ps, bp = next_ps()
                            S.op("pe", lambda: PE.matmul(ps[:, 0:257], lhsT=kk[k][0:n, :], rhs=v2[k][0:n, :], start=True, stop=True),
                                 reads=[Bck[k]], writes=[bp])
                            S.op("dve", lambda: V.scalar_tensor_tensor(out=CA[:, j, h, :], in0=CA[:, j, h, :], scalar=Fb[:, ci * 8 + h:ci * 8 + h + 1],
                                                                       in1=ps[:, 0:257], op0=ALU.mult, op1=ALU.add),
                                 reads=[BCA[j][h], BF, bp], writes=[BCA[j][h]])
                            S.op("act", lambda: A.copy(out=Cb[:], in_=CA[:, j, h, :]), reads=[BCA[j][h]], writes=[BCb])
                    if last:
                        S.op("dve", lambda: V.tensor_scalar(out=pco[:], in0=CA[:, j, h, :], scalar1=enmP[:, h:h + 1], scalar2=None, op0=ALU.mult),
                             reads=[BCA[j][h], BF], writes=[Bpco])
                        S.dma("sp", lambda: nc.sync.dma_start(out=pC[j, h], in_=pco[:, 0:256]), reads=[Bpco])
                        S.dma("act", lambda: A.dma_start(out=pn[j, h].rearrange("(k o) -> k o", o=1), in_=pco[:, 256:257]), reads=[Bpco])
                S.barrier()
            out_proj_and_ln1(tiles, HT, BHT, w_a_out[j], li, wp, Bwp, wrr)
            S.barrier()

    def chunk_mlp_mixer(blk, li, j, tiles, last):
        with ExitStack() as es:
            HT = sb(es, "HTb", [128, KC, TB], BF16); BHT = Buf("HT")
            wp = [sb(es, "wpb%d" % k, [128, KC, 256], BF16) for k in range(4)]
            Bwp = [Buf("wp%d" % k) for k in range(4)]
            wrr = [0]
            with ExitStack() as es2:
                ub = sb(es2, "ub", [128, BT + 1, D], BF16); Bub = [Buf("ub%d" % i) for i in range(BT + 1)]
                vb = sb(es2, "vb", [128, BT + 1, D], BF16); Bvb = [Buf("vb%d" % i) for i in range(BT + 1)]
                vr = sb(es2, "vr", [128, D]); Bvr = Buf("vr")
                bin_ = [sb(es2, "bin%d" % k, [1, 256]) for k in range(4)]; Bbin = [Buf("bin%d" % k) for k in range(4)]
                tmpv = [sb(es2, "tmpv%d" % k, [128, 256]) for k in range(2)]; Btv = [Buf("tmpv0"), Buf("tmpv1")]
                vs32 = sb(es2, "vs32", [64, D]); Bvs = Buf("vs32")
                stv = sb(es2, "stv", [128, BT + 1, 8, 6]); Bstv = [Buf("stv%d" % i) for i in range(BT + 1)]
                vnw = sb(es2, "vnw", [128, D]); Bvnw = Buf("vnw")
                wst = sb(es2, "wst", [128, 128]); Bwst = Buf("wst")
                wsT = sb(es2, "wsT", [128, 8, 128], BF16); BwsT = Buf("wsT")
                w44 = sb(es2, "w44", [4, 8, 4]); Bw44 = Buf("w44")
                bs = sb(es2, "bs", [128, 8]); Bbs = Buf("bs")
                st = sb(es2, "cst_", [128, 4, 6]); mv = sb(es2, "cmv", [128, 2]); rs = sb(es2, "crs", [128, 1]); Bw = Buf("cw")
                hid = sb(es2, "hid", [128, 256]); Bhid = Buf("hid")
                S.dma("act", lambda: A.dma_start(out=vnw[:], in_=row_bcast(b_norm_w[j], 128)), writes=[Bvnw])
                S.dma("sp", lambda: nc.sync.dma_start(out=bs[:], in_=b_b_s[j].rearrange("g t -> t g"), allow_slow_non_contiguous=True), writes=[Bbs])
                for g in range(8):
                    S.dma("sp", lambda: nc.sync.dma_start(out=wst[:], in_=w_b_s[j, g]), writes=[Bwst])
                    ps, bp = next_ps()
                    S.op("pe", lambda: PE.transpose(ps[:, 0:128], wst[:, :], ident[:, :]), reads=[Bwst, Bconst], writes=[bp])
                    S.op("dve", lambda: V.tensor_tensor(out=wsT[:, g, :], in0=ps[:, 0:128], in1=U128[:, :], op=ALU.mult),
                         reads=[bp, Bconst], writes=[BwsT])
                    S.op("dve", lambda: V.tensor_tensor(out=w44[0:4, g, :], in0=ps[0:4, 0:4], in1=U128[0:4, 0:4], op=ALU.mult),
                         reads=[bp, Bconst], writes=[Bw44])
                if last:
                    r4 = sb(es2, "r4", [64, 8, 4]); Br4 = Buf("r4")
                    wsS = sb(es2, "wsS", [64, 8, NS], BF16); BwsS = Buf("wsS")
                    bsS = sb(es2, "bsS", [64, 8]); BbsS = Buf("bsS")
                    ps, bp = next_ps()
                    S.op("pe", lambda: PE.matmul(ps[0:NS, 0:32], lhsT=RepT[0:4, 0:NS], rhs=w44[0:4, :, :].rearrange("p g t -> p (g t)"),
                                                 start=True, stop=True), reads=[Bconst, Bw44], writes=[bp])
                    S.op("dve", lambda: V.tensor_copy(out=r4[0:NS, :, :].rearrange("p g t -> p (g t)"), in_=ps[0:NS, 0:32]), reads=[bp], writes=[Br4])
                    ps, bp = next_ps()
                    S.op("pe", lambda: PE.matmul(ps[0:NS, 0:8], lhsT=RepT[0:4, 0:NS], rhs=bs[0:4, 0:8], start=True, stop=True),
                         reads=[Bconst, Bbs], writes=[bp])
                    S.op("dve", lambda: V.tensor_copy(out=bsS[0:NS, :], in_=ps[0:NS, 0:8]), reads=[bp], writes=[BbsS])
                    for g in range(8):
                        S.op("dve", lambda: V.tensor_tensor(out=wsS[0:NS, g, :].rearrange("p (b t) -> p b t", t=4),
                                                            in0=r4[0:NS, g, None, :].to_broadcast([NS, NSQ, 4]),
                                                            in1=maskS[0:NS, 0:NS].rearrange("p (b t) -> p b t", t=4), op=ALU.mult),
                             reads=[Br4, Bconst], writes=[BwsS])
                wb = w_b_in[j]
                NP = D // 256
                for cp in range(2 * NP):
                    s_ = wrr[0] % 4; wrr[0] += 1
                    load_panel(wp[s_], Bwp[s_], wb, cp * 256, 256)
                    S.dma("sp", lambda: nc.sync.dma_start(out=bin_[s_][:], in_=b_b_in[j, cp * 256:(cp + 1) * 256].rearrange("(o n) -> o n", o=1)),
                          writes=[Bbin[s_]])
                    for t in tiles:
                        i, rows, c0 = t["i"], t["rows"], t["c0"]
                        ps, bp = next_ps()
                        for kc in range(KC):
                            S.op("pe", lambda: PE.matmul(ps[0:rows, 0:256], lhsT=XT[:, kc, c0:c0 + rows], rhs=wp[s_][:, kc, 0:256],
                                                         start=(kc == 0), stop=False), reads=[Bwp[s_], BXT[i]], writes=[bp])
                        S.op("pe", lambda: PE.matmul(ps[0:rows, 0:256], lhsT=onesr[0:1, 0:rows], rhs=bin_[s_][0:1, :],
                                                     start=False, stop=True), reads=[Bbin[s_], Bconst], writes=[bp])
                        if cp < NP:
                            S.op("act", lambda: A.activation(out=ub[0:rows, i, cp * 256:(cp + 1) * 256], in_=ps[0:rows, 0:256],
                                                             func=AF.Gelu_apprx_tanh), reads=[bp], writes=[Bub[i]])
                        else:
                            cq = cp - NP
                            tv = (cq + i) % 2
                            S.op("act", lambda: A.activation(out=tmpv[tv][0:rows, :], in_=ps[0:rows, 0:256], func=AF.Gelu_apprx_tanh),
                                 reads=[bp], writes=[Btv[tv]])
                            S.op("dve", lambda: V.bn_stats(out=stv[0:rows, i, cq, :], in_=tmpv[tv][0:rows, :]), reads=[Btv[tv]], writes=[Bstv[i]])
                            if t["kind"] == "s":
                                S.op("pool", lambda: P.tensor_copy(out=vs32[0:rows, cq * 256:(cq + 1) * 256], in_=tmpv[tv][0:rows, :]),
                                     reads=[Btv[tv]], writes=[Bvs])
                            else:
                                S.op("pool", lambda: P.tensor_copy(out=vb[0:rows, i, cq * 256:(cq + 1) * 256], in_=tmpv[tv][0:rows, :]),
                                     reads=[Btv[tv]], writes=[Bvb[i]])
                for t in tiles:
                    i, rows = t["i"], t["rows"]
                    S.op("dve", lambda: V.bn_aggr(out=mv[0:rows, :], in_=stv[0:rows, i, :, :].rearrange("p a b -> p (a b)")),
                         reads=[Bstv[i]], writes=[Bw])
                    S.op("act", lambda: A.activation(out=rs[0:rows, :], in_=mv[0:rows, 1:2], func=AF.Sqrt, bias=LN_EPS, scale=1.0),
                         reads=[Bw], writes=[Bw])
                    S.op("dve", lambda: V.reciprocal(out=rs[0:rows, :], in_=rs[0:rows, :]), reads=[Bw], writes=[Bw])
                    if t["kind"] == "s":
                        S.op("dve", lambda: V.tensor_scalar(out=vs32[0:rows, :], in0=vs32[0:rows, :], scalar1=mv[0:rows, 0:1],
                                                            scalar2=rs[0:rows, 0:1], op0=ALU.subtract, op1=ALU.mult), reads=[Bvs, Bw], writes=[Bvs])
                        S.op("pool", lambda: P.tensor_tensor(out=vs32[0:rows, :], in0=vs32[0:rows, :], in1=vnw[0:rows, :], op=ALU.mult),
                             reads=[Bvs, Bvnw], writes=[Bvs])
                        S.op("act", lambda: A.copy(out=vb[0:rows, i, :], in_=vs32[0:rows, :]), reads=[Bvs], writes=[Bvb[i]])
                        S.dma("sp", lambda: nc.sync.dma_start(out=svo[j], in_=vs32[0:rows, :]), reads=[Bvs])
                        out_bufs.append(Bvs)
                    else:
                        S.op("dve", lambda: V.tensor_scalar(out=vb[0:rows, i, :], in0=vb[0:rows, i, :], scalar1=mv[0:rows, 0:1],
                                                            scalar2=rs[0:rows, 0:1], op0=ALU.subtract, op1=ALU.mult), reads=[Bvb[i], Bw], writes=[Bvb[i]])
                        S.op("pool", lambda: P.tensor_tensor(out=vb[0:rows, i, :], in0=vb[0:rows, i, :], in1=vnw[0:rows, :], op=ALU.mult),
                             reads=[Bvb[i], Bvnw], writes=[Bvb[i]])
                for t in tiles:
                    i, rows, c0 = t["i"], t["rows"], t["c0"]
                    for g in range(8):
                        ps, bp = next_ps()
                        if t["kind"] == "p":
                            S.op("pe", lambda: PE.matmul(ps[0:rows, 0:256], lhsT=wsT[:, g, :], rhs=vb[:, i, g * 256:(g + 1) * 256],
                                                         start=True, stop=True), reads=[BwsT, Bvb[i]], writes=[bp])
                            bsc = bs[0:rows, g:g + 1]; Bb_ = Bbs
                        else:
                            S.op("pe", lambda: PE.matmul(ps[0:rows, 0:256], lhsT=wsS[0:rows, g, 0:rows], rhs=vb[0:rows, i, g * 256:(g + 1) * 256],
                                                         start=True, stop=True), reads=[BwsS, Bvb[i]], writes=[bp])
                            bsc = bsS[0:rows, g:g + 1]; Bb_ = BbsS
                        S.op("dve", lambda: V.scalar_tensor_tensor(out=hid[0:rows, :], in0=ps[0:rows, 0:256], scalar=bsc,
                                                                   in1=ub[0:rows, i, g * 256:(g + 1) * 256], op0=ALU.add, op1=ALU.mult),
                             reads=[bp, Bb_, Bub[i]], writes=[Bhid])
                        transpose_tile(lambda kc: hid[0:rows, kc * 128:(kc + 1) * 128], rows, HT[:, 2 * g:2 * g + 2, :], c0, Bhid, BHT, nkc=2)
                S.barrier()
            out_proj_and_ln1(tiles, HT, BHT, w_b_out[j], li, wp, Bwp, wrr)
            S.barrier()

    def peer_ffn(blk, li, tiles):
        T = tiles[-1]["c0"] + tiles[-1]["rows"]
        groups = [(t0, min(512, T - t0)) for t0 in range(0, T, 512)]
        NTL = len(tiles)
        with ExitStack() as es:
            eidx = sb(es, "eidx", [128, BT + 1, 128], I32); Beidx = [Buf("eidx%d" % i) for i in range(BT + 1)]
            gate = sb(es, "gate", [128, BT + 1, 128]); Bgate = [Buf("gate%d" % i) for i in range(BT + 1)]
            with ExitStack() as es2:
                sc = sb(es2, "sc", [128, BT + 1, 16, 128]); Bsc = [Buf("sc%d" % i) for i in range(BT + 1)]
                wq = [sb(es2, "wq%d" % k, [128, KC, 128], BF16) for k in range(2)]; Bwq = [Buf("wq0"), Buf("wq1")]
                qpT = [sb(es2, "qpT%d" % k, [128, TB]) for k in range(2)]; BqpT = [Buf("qpT0"), Buf("qpT1")]
                kraw = sb(es2, "kraw", [128, 128]); Bkraw = Buf("kraw")
                keysT = sb(es2, "keysT", [128, 2, 128]); BkeysT = Buf("keysT")
                sv = sb(es2, "sv", [128, 16, 16]); si = sb(es2, "si", [128, 16, 16], U32); sif = sb(es2, "sif", [128, 16, 16])
                scr = sb(es2, "scr", [128, 256]); cand = sb(es2, "cand", [128, 8, 256])
                ts = sb(es2, "ts", [128, 8, 16]); pos = sb(es2, "pos", [128, 8, 16], U32)
                pa = sb(es2, "pa", [128, 8, 16], U32); pb = sb(es2, "pb", [128, 8, 16], U32)
                paf = sb(es2, "paf", [128, 8, 16]); pbf = sb(es2, "pbf", [128, 8, 16])
                eq = sb(es2, "eq", [128, 8, 16, 16]); ik = sb(es2, "ik", [128, 8, 16]); jk = sb(es2, "jk", [128, 8, 16])
                ef = sb(es2, "ef", [128, 128]); ex = sb(es2, "ex", [128, 8, 16]); zs = sb(es2, "zs", [128, 8])
                Br = Buf("route")
                for c in range(2):
                    S.dma("sp", lambda: nc.sync.dma_start(out=kraw[:], in_=peer_keys[li, c]), writes=[Bkraw])
                    ps, bp = next_ps()
                    S.op("pe", lambda: PE.transpose(ps[:, 0:128], kraw[:, :], ident[:, :]), reads=[Bkraw, Bconst], writes=[bp])
                    S.op("dve", lambda: V.tensor_copy(out=keysT[:, c, :], in_=ps[:, 0:128]), reads=[bp], writes=[BkeysT])
                for hc in range(16):
                    k = hc % 2
                    load_panel(wq[k], Bwq[k], peer_wq[li], hc * 128, 128)
                    for (t0, n) in groups:
                        ps, bp = next_ps()
                        for kc in range(KC):
                            S.op("pe", lambda: PE.matmul(ps[:, 0:n], lhsT=wq[k][:, kc, :], rhs=XT[:, kc, t0:t0 + n],
                                                         start=(kc == 0), stop=(kc == KC - 1)), reads=[Bwq[k]] + BXT, writes=[bp])
                        evac_copy(qpT[k][:, t0:t0 + n], ps[:, 0:n], [bp], [BqpT[k]])
                    for t in tiles:
                        i, rows, c0 = t["i"], t["rows"], t["c0"]
                        ps, bp = next_ps()
                        S.op("pe", lambda: PE.matmul(ps[0:rows, 0:128], lhsT=qpT[k][:, c0:c0 + rows], rhs=keysT[:, hc % 2, :],
                                                     start=True, stop=True), reads=[BqpT[k], BkeysT], writes=[bp])
                        evac_copy(sc[0:rows, i, hc, :], ps[0:rows, 0:128], [bp], [Bsc[i]])
                for t in tiles:
                    i, n = t["i"], t["rows"]
                    if n < 128:
                        S.op("pool", lambda: P.memset(eidx[:, i, :], 0), writes=[Beidx[i]])
                    for hc in range(16):
                        S.op("dve", lambda: V.max(out=sv[0:n, hc, 0:8], in_=sc[0:n, i, hc, :]), reads=[Bsc[i]], writes=[Br])
                        S.op("dve", lambda: V.max_index(out=si[0:n, hc, 0:8], in_max=sv[0:n, hc, 0:8], in_values=sc[0:n, i, hc, :]),
                             reads=[Bsc[i], Br], writes=[Br])
                        S.op("dve", lambda: V.match_replace(out=scr[0:n, 0:128], in_to_replace=sv[0:n, hc, 0:8], in_values=sc[0:n, i, hc, :],
                                                            imm_value=NEG), reads=[Bsc[i], Br], writes=[Br])
                        S.op("dve", lambda: V.max(out=sv[0:n, hc, 8:16], in_=scr[0:n, 0:128]), reads=[Br], writes=[Br])
                        S.op("dve", lambda: V.max_index(out=si[0:n, hc, 8:16], in_max=sv[0:n, hc, 8:16], in_values=scr[0:n, 0:128]),
                             reads=[Br], writes=[Br])
                    S.op("dve", lambda: V.tensor_copy(out=sif[0:n], in_=si[0:n]), reads=[Br], writes=[Br])
                    sv4 = sv[0:n].rearrange("p (h c) k -> p h c k", c=2)
                    sif4 = sif[0:n].rearrange("p (h c) k -> p h c k", c=2)
                    S.op("dve", lambda: V.tensor_tensor(out=cand[0:n].rearrange("p h (a b) -> p h a b", b=16),
                                                        in0=sv4[:, :, 0, :, None].to_broadcast([n, 8, 16, 16]),
                                                        in1=sv4[:, :, 1, None, :].to_broadcast([n, 8, 16, 16]), op=ALU.add),
                         reads=[Br], writes=[Br])
                    for h in range(8):
                        S.op("dve", lambda: V.max(out=ts[0:n, h, 0:8], in_=cand[0:n, h, :]), reads=[Br], writes=[Br])
                        S.op("dve", lambda: V.max_index(out=pos[0:n, h, 0:8], in_max=ts[0:n, h, 0:8], in_values=cand[0:n, h, :]),
                             reads=[Br], writes=[Br])
                        S.op("dve", lambda: V.match_replace(out=scr[0:n, :], in_to_replace=ts[0:n, h, 0:8], in_values=cand[0:n, h, :],
                                                            imm_value=NEG), reads=[Br], writes=[Br])
                        S.op("dve", lambda: V.max(out=ts[0:n, h, 8:16], in_=scr[0:n, :]), reads=[Br], writes=[Br])
                        S.op("dve", lambda: V.max_index(out=pos[0:n, h, 8:16], in_max=ts[0:n, h, 8:16], in_values=scr[0:n, :]),
                             reads=[Br], writes=[Br])
                    S.op("dve", lambda: V.tensor_single_scalar(out=pa[0:n], in_=pos[0:n], scalar=4, op=ALU.logical_shift_right),
                         reads=[Br], writes=[Br])
                    S.op("dve", lambda: V.tensor_single_scalar(out=pb[0:n], in_=pos[0:n], scalar=15, op=ALU.bitwise_and),
                         reads=[Br], writes=[Br])
                    S.op("dve", lambda: V.tensor_copy(out=paf[0:n], in_=pa[0:n]), reads=[Br], writes=[Br])
                    S.op("dve", lambda: V.tensor_copy(out=pbf[0:n], in_=pb[0:n]), reads=[Br], writes=[Br])
                    for (pf, cc, dst) in ((paf, 0, ik), (pbf, 1, jk)):
                        S.op("dve", lambda: V.tensor_tensor(out=eq[0:n], in0=iota16[0:n, None, None, :].to_broadcast([n, 8, 16, 16]),
                                                            in1=pf[0:n, :, :, None].to_broadcast([n, 8, 16, 16]), op=ALU.is_equal),
                             reads=[Br, Bconst], writes=[Br])
                        S.op("dve", lambda: V.tensor_tensor(out=eq[0:n], in0=eq[0:n],
                                                            in1=sif4[:, :, cc, None, :].to_broadcast([n, 8, 16, 16]), op=ALU.mult),
                             reads=[Br], writes=[Br])
                        S.op("dve", lambda: V.tensor_reduce(out=dst[0:n], in_=eq[0:n], axis=AX.X, op=ALU.add), reads=[Br], writes=[Br])
                    S.op("dve", lambda: V.scalar_tensor_tensor(out=ef[0:n, :], in0=ik[0:n].rearrange("p h k -> p (h k)"), scalar=128.0,
                                                               in1=jk[0:n].rearrange("p h k -> p (h k)"), op0=ALU.mult, op1=ALU.add),
                         reads=[Br], writes=[Br])
                    S.op("dve", lambda: V.tensor_scalar(out=ef[0:n, :], in0=ef[0:n, :], scalar1=float(li * NEXP), scalar2=None, op0=ALU.add),
                         reads=[Br], writes=[Br])
                    S.op("dve", lambda: V.tensor_copy(out=eidx[0:n, i, :], in_=ef[0:n, :]), reads=[Br], writes=[Beidx[i]])
                    S.op("dve", lambda: V.tensor_tensor(out=ex[0:n], in0=ts[0:n], in1=ts[0:n, :, 0:1].to_broadcast([n, 8, 16]), op=ALU.subtract),
                         reads=[Br], writes=[Br])
                    S.op("act", lambda: A.activation(out=ex[0:n], in_=ex[0:n], func=AF.Exp), reads=[Br], writes=[Br])
                    S.op("dve", lambda: V.tensor_reduce(out=zs[0:n], in_=ex[0:n], axis=AX.X, op=ALU.add), reads=[Br], writes=[Br])
                    S.op("dve", lambda: V.reciprocal(out=zs[0:n], in_=zs[0:n]), reads=[Br], writes=[Br])
                    S.op("dve", lambda: V.tensor_tensor(out=gate[0:n, i, :].rearrange("p (h k) -> p h k", k=16), in0=ex[0:n],
                                                        in1=zs[0:n, :, None].to_broadcast([n, 8, 16]), op=ALU.mult),
                         reads=[Br], writes=[Bgate[i]])
                S.barrier()
            with ExitStack() as es2:
                NU = 4
                ubf = [sb(es2, "ubf%d" % k, [128, D]) for k in range(NU)]; Bubf = [Buf("ubf%d" % k) for k in range(NU)]
                vbf = [sb(es2, "vbf%d" % k, [128, D]) for k in range(NU)]; Bvbf = [Buf("vbf%d" % k) for k in range(NU)]
                junk = sb(es2, "junk", [128, D], BF16); Bjunk = Buf("junk")
                acc = sb(es2, "acc", [128, D]); Bacc = Buf("acc")
                hraw = sb(es2, "hraw", [128, 128]); Bhraw = Buf("hraw")
                act = sb(es2, "actv", [128, 128]); Bact = Buf("actv")
                gt = sb(es2, "ln2_gt", [128, D]); bt = sb(es2, "ln2_bt", [128, D]); Bgb = Buf("ln2gb")
                st = sb(es2, "ln2_st", [128, 4, 6]); mv = sb(es2, "ln2_mv", [128, 2]); rs = sb(es2, "ln2_rs", [128, 1]); Bw = Buf("ln2w")
                S.dma("sp", lambda: nc.sync.dma_start(out=gt[:], in_=row_bcast(ln_g[li, 1], 128)), writes=[Bgb])
                S.dma("act", lambda: A.dma_start(out=bt[:], in_=row_bcast(ln_b[li, 1], 128)), writes=[Bgb])
                rr = [0]
                for t in tiles:
                    i, n = t["i"], t["rows"]
                    for s_ in range(128):
                        k = rr[0] % NU; rr[0] += 1
                        S.dma("pool", lambda: P.indirect_dma_start(out=ubf[k][:, :], out_offset=None, in_=peer_u_flat,
                                                                   in_offset=bass.IndirectOffsetOnAxis(ap=eidx[:, i, s_:s_ + 1], axis=0)),
                              reads=[Beidx[i]], writes=[Bubf[k]])
                        S.op("dve", lambda: V.scalar_tensor_tensor(out=junk[0:n, :], in0=ubf[k][0:n, :], scalar=1.0, in1=X[0:n, i, :],
                                                                   op0=ALU.mult, op1=ALU.mult, accum_out=hraw[0:n, s_:s_ + 1]),
                             reads=[Bubf[k], BX[i]], writes=[Bjunk, Bhraw])
                    S.op("act", lambda: A.activation(out=act[0:n, :], in_=hraw[0:n, :], func=AF.Gelu_apprx_tanh), reads=[Bhraw], writes=[Bact])
                    S.op("dve", lambda: V.tensor_tensor(out=act[0:n, :], in0=act[0:n, :], in1=gate[0:n, i, :], op=ALU.mult),
                         reads=[Bact, Bgate[i]], writes=[Bact])
                    for s_ in range(128):
                        k = rr[0] % NU; rr[0] += 1
                        S.dma("pool", lambda: P.indirect_dma_start(out=vbf[k][:, :], out_offset=None, in_=peer_v_flat,
                                                                   in_offset=bass.IndirectOffsetOnAxis(ap=eidx[:, i, s_:s_ + 1], axis=0)),
                              reads=[Beidx[i]], writes=[Bvbf[k]])
                        if s_ == 0:
                            S.op("dve", lambda: V.tensor_scalar(out=acc[0:n, :], in0=vbf[k][0:n, :], scalar1=act[0:n, 0:1], scalar2=None,
                                                                op0=ALU.mult), reads=[Bvbf[k], Bact], writes=[Bacc])
                        else:
                            S.op("dve", lambda: V.scalar_tensor_tensor(out=acc[0:n, :], in0=vbf[k][0:n, :], scalar=act[0:n, s_:s_ + 1],
                                                                       in1=acc[0:n, :], op0=ALU.mult, op1=ALU.add),
                                 reads=[Bvbf[k], Bact, Bacc], writes=[Bacc])
                    S.op("dve", lambda: V.scalar_tensor_tensor(out=X[0:n, i, :], in0=X[0:n, i, :], scalar=ALPHA, in1=acc[0:n, :],
                                                               op0=ALU.mult, op1=ALU.add), reads=[BX[i], Bacc], writes=[BX[i]])
                    layer_norm_tile(i, n, gt, bt, Bgb, (st, mv, rs, Bw))
                S.barrier()

    def ple_stage(blk, li, tiles, tok0):
        with ExitStack() as es:
            wp = [sb(es, "wpp%d" % k, [128, KC, 256], BF16) for k in range(3)]; Bwp = [Buf("wpp%d" % k) for k in range(3)]
            we = [sb(es, "we%d" % k, [128, 2, 256], BF16) for k in range(3)]; Bwe = [Buf("we%d" % k) for k in range(3)]
            praw = sb(es, "praw", [128, PLE]); Bpraw = Buf("praw")
            pT = sb(es, "pT", [128, 2, TB], BF16); BpT = Buf("pT")
            sg = [sb(es, "sg%d" % k, [128, 256]) for k in range(2)]; Bsg = [Buf("sg0"), Buf("sg1")]
            for t in tiles:
                i, rows, c0 = t["i"], t["rows"], t["c0"]
                src = pp[li, tok0 + i * 128: tok0 + i * 128 + rows, :] if t["kind"] == "p" else pps[li, 0:rows, :]
                S.dma("sp", lambda: nc.sync.dma_start(out=praw[0:rows, :], in_=src), writes=[Bpraw])
                transpose_tile(lambda kc: praw[0:rows, kc * 128:(kc + 1) * 128], rows, pT, c0, Bpraw, BpT, nkc=2)
            rr = 0
            for cp in range(D // 256):
                s_ = rr % 3; rr += 1
                load_panel(wp[s_], Bwp[s_], ple_gate_w[li], cp * 256, 256)
                load_panel(we[s_], Bwe[s_], ple_w[li], cp * 256, 256, nkc=2)
                for t in tiles:
                    i, rows, c0 = t["i"], t["rows"], t["c0"]
                    ps, bp = next_ps()
                    for kc in range(KC):
                        S.op("pe", lambda: PE.matmul(ps[0:rows, 0:256], lhsT=XT[:, kc, c0:c0 + rows], rhs=wp[s_][:, kc, 0:256],
                                                     start=(kc == 0), stop=(kc == KC - 1)), reads=[Bwp[s_], BXT[i]], writes=[bp])
                    k = (cp + i) % 2
                    S.op("act", lambda: A.activation(out=sg[k][0:rows, :], in_=ps[0:rows, 0:256], func=AF.Sigmoid), reads=[bp], writes=[Bsg[k]])
                    ps2, bp2 = next_ps()
                    for kc in range(2):
                        S.op("pe", lambda: PE.matmul(ps2[0:rows, 0:256], lhsT=pT[:, kc, c0:c0 + rows], rhs=we[s_][:, kc, 0:256],
                                                     start=(kc == 0), stop=(kc == 1)), reads=[Bwe[s_], BpT], writes=[bp2])
                    S.op("dve", lambda: V.tensor_tensor(out=sg[k][0:rows, :], in0=sg[k][0:rows, :], in1=ps2[0:rows, 0:256], op=ALU.mult),
                         reads=[Bsg[k], bp2], writes=[Bsg[k]])
                    S.op("pool", lambda: P.tensor_tensor(out=X[0:rows, i, cp * 256:(cp + 1) * 256], in0=X[0:rows, i, cp * 256:(cp + 1) * 256],
                                                         in1=sg[k][0:rows, :], op=ALU.add), reads=[BX[i], Bsg[k]], writes=[BX[i]])
            S.barrier()

    Bxout = Buf("xout")
    for blk in range(NBLK):
        last = blk == NBLK - 1
        tok0 = blk * BT * 128
        tiles = [dict(i=i, rows=128, c0=i * 128, kind="p") for i in range(BT)]
        if last:
            tiles.append(dict(i=BT, rows=NS, c0=BT * 128, kind="s"))
        chunks = []
        for i in range(BT):
            chunks.append(dict(c0=i * 128, n=64, kind="p"))
            chunks.append(dict(c0=i * 128 + 64, n=64, kind="p"))
        if last:
            chunks.append(dict(c0=BT * 128, n=NS, kind="s"))
        for t in tiles:
            i, rows = t["i"], t["rows"]
            src = xp[tok0 + i * 128: tok0 + i * 128 + rows, :] if t["kind"] == "p" else xs[0:rows, :]
            S.dma("sp" if i % 2 == 0 else "act", (lambda: nc.sync.dma_start(out=X[0:rows, i, :], in_=src)) if i % 2 == 0
                  else (lambda: A.dma_start(out=X[0:rows, i, :], in_=src)), writes=[BX[i]])
        for li in range(DEPTH):
            j = li // 2
            build_XT(tiles)
            if li % 2 == 0:
                mlstm_mixer(blk, li, j, tiles, chunks, last)
            else:
                chunk_mlp_mixer(blk, li, j, tiles, last)
            build_XT(tiles)
            peer_ffn(blk, li, tiles)
            build_XT(tiles)
            ple_stage(blk, li, tiles, tok0)
        for t in tiles:
            i, rows = t["i"], t["rows"]
            dst = yp[tok0 + i * 128: tok0 + i * 128 + rows, :] if t["kind"] == "p" else ys[0:rows, :]
            S.dma("sp", lambda: nc.sync.dma_start(out=dst, in_=X[0:rows, i, :]), reads=[BX[i]])
        S.barrier()
    S.barrier()
    top.close()
    return nc, S


def _in_maps(cfg, inp):
    NCORE, NSEQ, SEQ, NSQ, DEPTH = cfg["NCORE"], cfg["NSEQ"], cfg["SEQ"], cfg["NSQ"], cfg["DEPTH"]
    f = lambda a: np.ascontiguousarray(np.asarray(a, dtype=np.float32))
    shared = {k: f(inp[k]) for k in ("w_a_in", "b_a_gate", "a_norm_w", "w_a_out", "w_b_in", "b_b_in", "b_norm_w", "w_b_s", "b_b_s",
                                     "w_b_out", "ln_g", "ln_b", "peer_wq", "peer_keys", "peer_u", "peer_v", "ple_w", "ple_gate_w")}
    xprompt, xsample = f(inp["x_prompt"]), f(inp["x_sample"])
    sC, sn, sm = f(inp["state_C"]), f(inp["state_n"]), f(inp["state_m"])
    ppr, psa = f(inp["p_prompt"]), f(inp["p_sample"])
    maps = []
    for c in range(NCORE):
        b = c % NSEQ
        s0, s1 = c * NSQ, (c + 1) * NSQ
        m = dict(shared)
        m["xp"] = f(xprompt[b])
        m["xs"] = f(xsample[s0:s1].reshape(NSQ * 4, D))
        m["sC"] = f(sC[:, s0:s1]); m["sn"] = f(sn[:, s0:s1]); m["sm"] = f(sm[:, s0:s1])
        m["pp"] = f(ppr[:, b]); m["pps"] = f(psa[:, s0:s1].reshape(DEPTH, NSQ * 4, PLE))
        maps.append(m)
    return maps


def run_cfg(cfg, inp, trace=False):
    nc, S = build_program(cfg)
    NCORE, NSEQ, SEQ, NSQ, DEPTH = cfg["NCORE"], cfg["NSEQ"], cfg["SEQ"], cfg["NSQ"], cfg["DEPTH"]
    res = run_bass_kernel_spmd(nc, _in_maps(cfg, inp), core_ids=list(range(NCORE)))
    R = res.results
    NA = (DEPTH + 1) // 2
    y_prompt = np.stack([R[b]["yp"] for b in range(NSEQ)], 0)
    y_sample = np.concatenate([R[c]["ys"].reshape(NSQ, 4, D) for c in range(NCORE)], 0)
    prompt_C = np.stack([R[b]["pC"] for b in range(NSEQ)], 1)
    prompt_n = np.stack([R[b]["pn"] for b in range(NSEQ)], 1)
    prompt_m = np.stack([R[b]["pm"] for b in range(NSEQ)], 1)
    sample_C = np.concatenate([R[c]["sCo"] for c in range(NCORE)], 1)
    sample_n = np.concatenate([R[c]["sno"] for c in range(NCORE)], 1)
    sample_m = np.concatenate([R[c]["smo"] for c in range(NCORE)], 1)
    sample_v = np.concatenate([R[c]["svo"].reshape(-1, NSQ, 4, D) for c in range(NCORE)], 1)
    outs = (y_prompt, y_sample, prompt_C, prompt_n, prompt_m, sample_C, sample_n, sample_m, sample_v)
    return tuple(np.ascontiguousarray(o, dtype=np.float32) for o in outs)


FULL_CFG = dict(NCORE=8, NSEQ=4, SEQ=2048, BT=4, NSQ=16, DEPTH=4)


def kernel(**inputs):
    return run_cfg(FULL_CFG, inputs)
```
